# Optimizing a Trainium2 kernel written in Bass

```python
import jax, jax.numpy as jnp
from jax import lax
import numpy as np

D_MODEL = 1024
BATCH = 2
SEQ = 16384
DEPTH = 4

BLOCK = 128
EPS = 1e-6
NEG = -1e30
GM_GROUPS = 4
GM_GROUP_DIM = 128
GM_WIDTH = GM_GROUPS * GM_GROUP_DIM
HEAD_DIM = 64
SW_HEADS = 8
SW_KV_HEADS = 2
SW_WIDTH = SW_HEADS * HEAD_DIM
SW_KV_WIDTH = SW_KV_HEADS * HEAD_DIM
WINDOW = 128
ROPE_THETA = 10000.0
SB_HEADS = 4
SB_HEAD_DIM = 128
SB_WIDTH = SB_HEADS * SB_HEAD_DIM
N_BRANCH = 3
IN_SPLITS = (
    GM_WIDTH, GM_WIDTH, GM_WIDTH,
    SW_WIDTH, SW_KV_WIDTH, SW_KV_WIDTH, SW_WIDTH,
    SB_WIDTH, SB_WIDTH, SB_WIDTH, SB_WIDTH,
    N_BRANCH * D_MODEL,
)
IN_WIDTH = sum(IN_SPLITS)

kernel_name = "hybrid_gmlp_swa_sink_stickbreak_block"


def rms_norm(x, g):
    xf = x.astype(jnp.float32)
    y = xf * lax.rsqrt(jnp.mean(xf * xf, axis=-1, keepdims=True) + EPS)
    return (y * g.astype(jnp.float32)).astype(x.dtype)


def rope(x, pos):
    half = HEAD_DIM // 2
    freqs = ROPE_THETA ** (-jnp.arange(half, dtype=jnp.float32) / half)
    ang = pos.astype(jnp.float32)[:, None] * freqs[None, :]
    cos = jnp.cos(ang)[None, :, None, :]
    sin = jnp.sin(ang)[None, :, None, :]
    xf = x.astype(jnp.float32)
    x1, x2 = xf[..., :half], xf[..., half:]
    out = jnp.concatenate([x1 * cos - x2 * sin, x2 * cos + x1 * sin], axis=-1)
    return out.astype(x.dtype)


def chunk_gmlp(u, v, w_s, b_s, g_v):
    B, S, _ = v.shape
    n = S // BLOCK
    vf = v.astype(jnp.float32)
    mu = jnp.mean(vf, axis=-1, keepdims=True)
    var = jnp.mean(jnp.square(vf - mu), axis=-1, keepdims=True)
    vn = ((vf - mu) * lax.rsqrt(var + EPS) * g_v.astype(jnp.float32)).astype(v.dtype)
    vn = vn.reshape(B, n, BLOCK, GM_GROUPS, GM_GROUP_DIM)
    causal = jnp.tril(jnp.ones((BLOCK, BLOCK), dtype=bool))
    w = jnp.where(causal[None], w_s, jnp.zeros_like(w_s))
    mixed = jnp.einsum('gts,bnsgc->bntgc', w, vn)
    mixed = mixed + b_s.T[None, None, :, :, None]
    return u * mixed.reshape(B, S, GM_WIDTH)


def sliding_window_attention(q, k, v, sinks):
    B, S, H, Dh = q.shape
    n = S // BLOCK
    rep = H // SW_KV_HEADS
    qb = q.reshape(B, n, BLOCK, SW_KV_HEADS, rep, Dh)

    def band(t):
        tb = t.reshape(B, n, BLOCK, SW_KV_HEADS, Dh)
        prev = jnp.pad(tb[:, :-1], ((0, 0), (1, 0), (0, 0), (0, 0), (0, 0)))
        return jnp.concatenate([prev, tb], axis=2)

    kb, vb = band(k), band(v)
    s = jnp.einsum('bnqgrd,bnkgd->bngrqk', qb, kb).astype(jnp.float32) * (Dh ** -0.5)
    qi = jnp.arange(BLOCK)[:, None]
    kj = jnp.arange(2 * BLOCK)[None, :]
    diff = qi + BLOCK - kj
    local = (diff >= 0) & (diff < WINDOW)
    blk = jnp.arange(n)[:, None, None]
    valid = local[None] & ((blk > 0) | (kj >= BLOCK)[None])
    s = jnp.where(valid[None, :, None, None], s, NEG)
    sink = jnp.broadcast_to(
        sinks.astype(jnp.float32).reshape(1, 1, SW_KV_HEADS, rep, 1, 1), s.shape[:-1] + (1,))
    p = jax.nn.softmax(jnp.concatenate([s, sink], axis=-1), axis=-1)[..., :-1]
    o = jnp.einsum('bngrqk,bnkgd->bnqgrd', p.astype(v.dtype), vb)
    return o.reshape(B, S, H * Dh)


def stick_breaking_attention(q, k, v):
    B, S, H, Dh = q.shape
    n = S // BLOCK
    qt = q.transpose(0, 2, 1, 3) * (Dh ** -0.5)
    kt = k.transpose(0, 2, 1, 3)
    vt = v.transpose(0, 2, 1, 3)
    qoff = jnp.arange(BLOCK)
    outs = []
    for i in range(n):
        kl = (i + 1) * BLOCK
        z = jnp.einsum('bhqd,bhkd->bhqk', qt[:, :, i * BLOCK:kl],
                       kt[:, :, :kl]).astype(jnp.float32)
        before = (jnp.arange(kl)[None, :] < (i * BLOCK + qoff)[:, None])[None, None]
        log_fail = jnp.where(before, jax.nn.log_sigmoid(-z), 0.0)
        a = jnp.where(before, jnp.exp(z + lax.cumsum(log_fail, axis=3, reverse=True)), 0.0)
        outs.append(jnp.einsum('bhqk,bhkd->bhqd', a.astype(v.dtype), vt[:, :, :kl]))
    o = jnp.concatenate(outs, axis=2)
    return o.transpose(0, 2, 1, 3).reshape(B, S, H * Dh)


def hybrid_layer(x, pos, w_in, gm_w_s, gm_b_s, gm_norm_gain, sw_sinks,
                 w_branch_a, w_branch_b, w_branch_c, b_merge, w_out, g_pre, g_post):
    B, S, _ = x.shape
    h = rms_norm(x, g_pre)
    p = h @ w_in
    offsets = np.cumsum(np.array(IN_SPLITS))[:-1].tolist()
    (u_a, v_a, gate_a, q_b, k_b, v_b, gate_b,
     q_c, k_c, v_c, gate_c, merge_logits) = jnp.split(p, offsets, axis=-1)

    y_a = chunk_gmlp(u_a, v_a, gm_w_s, gm_b_s, gm_norm_gain) * jax.nn.silu(gate_a)

    qh = rope(q_b.reshape(B, S, SW_HEADS, HEAD_DIM), pos)
    kh = rope(k_b.reshape(B, S, SW_KV_HEADS, HEAD_DIM), pos)
    vh = v_b.reshape(B, S, SW_KV_HEADS, HEAD_DIM)
    y_b = sliding_window_attention(qh, kh, vh, sw_sinks) * jax.nn.silu(gate_b)

    y_c = stick_breaking_attention(q_c.reshape(B, S, SB_HEADS, SB_HEAD_DIM),
                                   k_c.reshape(B, S, SB_HEADS, SB_HEAD_DIM),
                                   v_c.reshape(B, S, SB_HEADS, SB_HEAD_DIM)) * jax.nn.silu(gate_c)

    gates = jax.nn.sigmoid(merge_logits.reshape(B, S, N_BRANCH, D_MODEL) + b_merge)
    merged = (gates[:, :, 0] * (y_a @ w_branch_a)
              + gates[:, :, 1] * (y_b @ w_branch_b)
              + gates[:, :, 2] * (y_c @ w_branch_c))
    out = merged @ w_out
    return x + rms_norm(out, g_post)


def setup_inputs(seed: int = 0) -> dict:
    key = jax.random.key(seed)
    ks = jax.random.split(key, 16)
    f32 = jnp.float32
    nrm = lambda k, shape, s: jax.random.normal(k, shape, f32) * s
    return {
        "x": nrm(ks[0], (BATCH, SEQ, D_MODEL), 1.0),
        "w_in": nrm(ks[1], (DEPTH, D_MODEL, IN_WIDTH), D_MODEL ** -0.5),
        "gm_w_s": nrm(ks[2], (DEPTH, GM_GROUPS, BLOCK, BLOCK), BLOCK ** -0.5),
        "gm_b_s": 1.0 + nrm(ks[3], (DEPTH, GM_GROUPS, BLOCK), 0.02),
        "gm_norm_gain": 1.0 + nrm(ks[4], (DEPTH, GM_WIDTH), 0.02),
        "sw_sinks": nrm(ks[5], (DEPTH, SW_HEADS), 1.0),
        "w_branch_a": nrm(ks[6], (DEPTH, GM_WIDTH, D_MODEL), GM_WIDTH ** -0.5),
        "w_branch_b": nrm(ks[7], (DEPTH, SW_WIDTH, D_MODEL), SW_WIDTH ** -0.5),
        "w_branch_c": nrm(ks[8], (DEPTH, SB_WIDTH, D_MODEL), SB_WIDTH ** -0.5),
        "b_merge": nrm(ks[9], (DEPTH, N_BRANCH, D_MODEL), 0.02),
        "w_out": nrm(ks[10], (DEPTH, D_MODEL, D_MODEL), D_MODEL ** -0.5),
        "g_pre": 1.0 + nrm(ks[11], (DEPTH, D_MODEL), 0.02),
        "g_post": 1.0 + nrm(ks[12], (DEPTH, D_MODEL), 0.02),
    }


def reference(x, w_in, gm_w_s, gm_b_s, gm_norm_gain, sw_sinks, w_branch_a, w_branch_b,
              w_branch_c, b_merge, w_out, g_pre, g_post):
    pos = jnp.arange(x.shape[1])
    for l in range(DEPTH):
        x = hybrid_layer(x, pos, w_in[l], gm_w_s[l], gm_b_s[l], gm_norm_gain[l], sw_sinks[l],
                         w_branch_a[l], w_branch_b[l], w_branch_c[l], b_merge[l], w_out[l],
                         g_pre[l], g_post[l])
    return x
```

```python
import os
import numpy as np
import ml_dtypes
import concourse.bass as bass
import concourse.mybir as mybir
from concourse.bass_utils import run_bass_kernel_spmd
F32 = mybir.dt.float32; BF16 = mybir.dt.bfloat16
AF = mybir.ActivationFunctionType; ALU = mybir.AluOpType; AX = mybir.AxisListType


class Sched:
    def __init__(self, nc):
        self.nc = nc
        self.eng = {"pe": nc.tensor, "act": nc.scalar, "dve": nc.vector, "pool": nc.gpsimd, "sp": nc.sync}
        self._ctx = []
        self.sem = {}
        self.cnt = {}
        for e in self.eng:
            self.sem[e] = self._newsem("done_" + e)
            self.cnt[e] = 0
        self.waited = {e: {} for e in self.eng}
        self.lastw = {}
        self.lastr = {}
        self.dsem = {}
        self.semobj = dict(self.sem)
        self.n_inst = 0

    def _newsem(self, name):
        cm = self.nc.semaphore(name)
        s = cm.__enter__()
        self._ctx.append(cm)
        return s

    def _deps(self, reads, writes):
        deps = {}
        def add(ev):
            if ev is None:
                return
            k, v = ev
            if deps.get(k, 0) < v:
                deps[k] = v
        for b in reads:
            add(self.lastw.get(b))
        for b in writes:
            add(self.lastw.get(b))
            for k, v in self.lastr.get(b, {}).items():
                add((k, v))
        return deps

    def _wait(self, e, deps, skip_self=False):
        w = self.waited[e]
        for k, v in deps.items():
            if skip_self and k == e:
                continue
            if w.get(k, 0) >= v:
                continue
            self.eng[e].wait_ge(self.semobj[k], v)
            w[k] = v

    def _record(self, ev, reads, writes):
        k, v = ev
        for b in writes:
            self.lastw[b] = ev
            self.lastr[b] = {}
        for b in reads:
            d = self.lastr.setdefault(b, {})
            if d.get(k, 0) < v:
                d[k] = v

    def op(self, e, fn, reads=(), writes=()):
        deps = self._deps(reads, writes)
        self._wait(e, deps, skip_self=(e == "pe"))
        inst = fn(self.eng[e])
        self.cnt[e] += 1
        inst.then_inc(self.sem[e], 1)
        self._record((e, self.cnt[e]), reads, writes)
        self.n_inst += 1
        return inst

    def dma(self, q, out, in_, reads=(), writes=(), key=None, **kw):
        if key is None:
            key = (list(writes) + list(reads))[0]
        sk = "dma:" + str(key)
        if sk not in self.semobj:
            self.semobj[sk] = self._newsem("d" + str(len(self.semobj)))
            self.cnt[sk] = 0
        deps = self._deps(reads, writes)
        self._wait(q, deps)
        inst = self.eng[q].dma_start(out=out, in_=in_, **kw)
        self.cnt[sk] += 16
        inst.then_inc(self.semobj[sk], 16)
        self._record((sk, self.cnt[sk]), reads, writes)
        self.n_inst += 1
        return inst

    def finish(self, e="sp"):
        deps = {}
        for k in self.semobj:
            if self.cnt.get(k, 0) > 0:
                deps[k] = self.cnt[k]
        self._wait(e, deps)


NS = 4


def p2_consts():
    j = np.arange(128)
    negL = -(j[:, None] >= j[None, :]).astype(np.float32)
    M = (j[:, None] < j[None, :]).astype(np.float32)
    negones = -np.ones((1, 128), np.float32)
    bf = ml_dtypes.bfloat16
    return {"c_negL": negL.astype(bf), "c_mask": M.astype(bf), "c_negones": negones.astype(bf)}


def stream_groups(NG):
    streams = [[] for _ in range(NS)]
    for base in range(0, NG, 2 * NS):
        for s in range(NS):
            a = base + s
            b = base + 2 * NS - 1 - s
            if a < NG: streams[s].append(a)
            if b < NG and b != a: streams[s].append(b)
    return streams


def emit_p2(nc, sc, S, qT_d, kT_d, v_d, oT_d, consts, tensors):
    NB = S // 128
    NG = S // 512
    qT, kT, v, negL, mask, negones, E, SP, A, z0, Rb, Ost, Z, O = tensors
    CH = 2048 if S >= 2048 else S
    for c in range(0, S, CH):
        sc.dma("sp", qT[:, c:c + CH], qT_d[:, c:c + CH], writes=[("qT", c // CH)])
        sc.dma("act", kT[:, c:c + CH], kT_d[:, c:c + CH], writes=[("kT", c // CH)])
        sc.dma("pool", v[:, c // 128:(c + CH) // 128, :],
               v_d[c:c + CH, :].rearrange("(n p) d -> p n d", p=128), writes=[("v", c // CH)])
    sc.dma("sp", negL[:], consts["c_negL"], writes=["negL"])
    sc.dma("sp", mask[:], consts["c_mask"], writes=["mask"])
    sc.dma("sp", negones[:], consts["c_negones"], writes=["negones"])

    streams = stream_groups(NG)
    tiles = []
    for s in range(NS):
        lst = []
        for G in streams[s]:
            kbs = list(range(4 * G + 3, -1, -1))
            for i, kb in enumerate(kbs):
                m = kb - 4 * G
                c0 = max(0, m) * 128
                lst.append(dict(G=G, kb=kb, c0=c0, diag=(m >= 0), first=(i == 0), last=(i == len(kbs) - 1),
                                prev_c0=(None if i == 0 else max(0, kbs[i - 1] - 4 * G) * 128)))
        tiles.append(lst)
    nr = max(len(l) for l in tiles)

    def T(s, r):
        return tiles[s][r] if r < len(tiles[s]) else None

    def phaseA(s, t):
        G, kb, c0 = t["G"], t["kb"], t["c0"]
        sc.op("pe", lambda e: e.matmul(Z[s][:, c0:512], lhsT=kT[:, kb * 128:(kb + 1) * 128],
                                       rhs=qT[:, G * 512 + c0:(G + 1) * 512], start=True, stop=False),
              reads=[("kT", kb * 128 // CH), ("qT", G * 512 // CH), "negL"], writes=[("Z", s)])

    for s in range(NS):
        if T(s, 0): phaseA(s, T(s, 0))
    for r in range(nr):
        for s in range(NS):
            t = T(s, r)
            if not t: continue
            c0 = t["c0"]
            sc.op("act", lambda e: e.activation(out=E[s][:, c0:512], in_=Z[s][:, c0:512], func=AF.Exp),
                  writes=[("E", s), ("Z", s)])
            sc.op("dve", lambda e: e.tensor_copy(out=z0[s][0:1, c0:512], in_=Z[s][0:1, c0:512]),
                  writes=[("z0", s), ("Z", s)])
            sc.op("act", lambda e: e.activation(out=SP[s][:, c0:512], in_=E[s][:, c0:512], func=AF.Ln, bias=1.0),
                  reads=[("E", s)], writes=[("SP", s)])
            if t["diag"]:
                sc.op("dve", lambda e: e.tensor_tensor(out=SP[s][:, c0:c0 + 128], in0=SP[s][:, c0:c0 + 128],
                                                       in1=mask[:], op=ALU.mult),
                      reads=[("SP", s), "mask"], writes=[("SP", s)])
        for s in range(NS):
            t = T(s, r)
            if not t: continue
            c0 = t["c0"]
            carry = not t["first"]
            sc.op("pe", lambda e: e.matmul(Z[s][:, c0:512], lhsT=negL[:], rhs=SP[s][:, c0:512],
                                           start=False, stop=not carry),
                  reads=[("SP", s), "negL", ("z0", s)], writes=[("Z", s)])
            if carry:
                pc0 = t["prev_c0"]
                sc.op("pe", lambda e: e.matmul(Z[s][:, pc0:512], lhsT=negones[0:1, :], rhs=Rb[s][0:1, pc0:512],
                                               start=False, stop=True),
                      reads=[("Rb", s), "negones"], writes=[("Z", s)])
        for s in range(NS):
            t = T(s, r)
            if not t: continue
            c0 = t["c0"]
            sc.op("act", lambda e: e.activation(out=A[s][:, c0:512], in_=Z[s][:, c0:512], func=AF.Exp),
                  writes=[("A", s), ("Z", s)])
            if not t["last"]:
                sc.op("dve", lambda e: e.tensor_tensor(out=Rb[s][0:1, c0:512], in0=z0[s][0:1, c0:512],
                                                       in1=Z[s][0:1, c0:512], op=ALU.subtract),
                      reads=[("z0", s)], writes=[("Rb", s), ("Z", s)])
            if t["diag"]:
                sc.op("dve", lambda e: e.tensor_tensor(out=A[s][:, c0:c0 + 128], in0=A[s][:, c0:c0 + 128],
                                                       in1=mask[:], op=ALU.mult),
                      reads=[("A", s), "mask"], writes=[("A", s)])
        for s in range(NS):
            t = T(s, r)
            if not t: continue
            c0, kb, G = t["c0"], t["kb"], t["G"]
            vk = v[:, kb, :]
            rd = [("A", s), ("v", kb * 128 // CH)]
            if t["first"]:
                sc.op("dve", lambda e: e.memset(O[s][:, :], 0.0), writes=[("O", s)])
            sc.op("pe", lambda e: e.matmul(O[s][:, c0:512], lhsT=vk, rhs=A[s][:, c0:512], start=False,
                                           stop=t["last"], skip_group_check=True),
                  reads=rd, writes=[("O", s)])
            if t["last"]:
                ob = Ost[s % 2]
                sc.op("dve", lambda e: e.tensor_copy(out=ob[:], in_=O[s][:, :]),
                      writes=[("Ost", s % 2), ("O", s)])
                sc.dma("sp", oT_d[:, G * 512:(G + 1) * 512], ob[:], reads=[("Ost", s % 2)], writes=[("oT_d", G)], key=("Ost", s % 2))
            tn = T(s, r + 1)
            if tn: phaseA(s, tn)


def build_p2(S):
    nc = bass.Bass("TRN2", target_bir_lowering=False)
    qT_d = nc.dram_tensor("qT", [128, S], BF16, kind="ExternalInput").ap()
    kT_d = nc.dram_tensor("kT", [128, S], BF16, kind="ExternalInput").ap()
    v_d = nc.dram_tensor("v", [S, 128], BF16, kind="ExternalInput").ap()
    oT_d = nc.dram_tensor("oT", [128, S], BF16, kind="ExternalOutput").ap()
    consts = {k: nc.dram_tensor(k, list(a.shape), BF16, kind="ExternalInput").ap() for k, a in p2_consts().items()}
    ctx = []
    def sb(name, shape, dt):
        cm = nc.sbuf_tensor(name, shape, dt); t = cm.__enter__(); ctx.append(cm); return t
    def ps(name, shape, dt):
        cm = nc.psum_tensor(name, shape, dt); t = cm.__enter__(); ctx.append(cm); return t
    qT = sb("qTs", [128, S], BF16); kT = sb("kTs", [128, S], BF16); v = sb("vs", [128, S // 128, 128], BF16)
    negL = sb("negL", [128, 128], BF16); mask = sb("mask", [128, 128], BF16); negones = sb("negones", [1, 128], BF16)
    E = [sb(f"E{s}", [128, 512], F32) for s in range(NS)]
    SP = [sb(f"SP{s}", [128, 512], BF16) for s in range(NS)]
    A = [sb(f"A{s}", [128, 512], BF16) for s in range(NS)]
    z0 = [sb(f"z0{s}", [1, 512], F32) for s in range(NS)]
    Rb = [sb(f"Rb{s}", [1, 512], BF16) for s in range(NS)]
    Ost = [sb(f"Ost{s}", [128, 512], BF16) for s in range(2)]
    Z = [ps(f"Z{s}", [128, 512], F32) for s in range(NS)]
    O = [ps(f"O{s}", [128, 512], F32) for s in range(NS)]
    sc = Sched(nc)
    emit_p2(nc, sc, S, qT_d, kT_d, v_d, oT_d, consts, (qT, kT, v, negL, mask, negones, E, SP, A, z0, Rb, Ost, Z, O))
    sc.finish("sp")
    return nc, sc


D = 1024; INW = 7936
U_OFF, V_OFF, GA_OFF, QB_OFF, KB_OFF, GB_OFF = 0, 512, 1024, 1536, 2048, 2304
QC_OFF, KC_OFF, VC_OFF, GC_OFF, MG_OFF = 2816, 3328, 3840, 4352, 4864
EPS = 1e-6
NEGM = -30000.0
F3ALL = ["F3all", "F3k", "F3kr", "F3o", "F3ta", "F3tb"]
bf = ml_dtypes.bfloat16


def tok_consts():
    j = np.arange(128)
    ident = np.eye(128, dtype=np.float32)
    triu = (j[:, None] <= j[None, :]).astype(np.float32)
    ones_row = np.ones((1, 128), np.float32)
    q = j[:, None]; kk = j[None, :]
    m_prev = np.where(kk > q, 0.0, NEGM); m_this = np.where(kk <= q, 0.0, NEGM)
    mask_std = np.concatenate([m_prev, m_this], 1).astype(np.float32)
    mask_first = np.concatenate([np.full((128, 128), NEGM), m_this], 1).astype(np.float32)
    return {"c_ident": ident.astype(bf), "c_triu": triu.astype(bf), "c_ones": ones_row.astype(bf),
            "c_mask_std": mask_std, "c_mask_first": mask_first}


def rope_table(pos):
    half = 32
    freqs = (10000.0 ** (-np.arange(half, dtype=np.float32) / half)).astype(np.float32)
    ang = pos.astype(np.float32)[:, None] * freqs[None, :]
    return np.concatenate([np.cos(ang), np.sin(ang)], 1).astype(np.float32)


class Alloc:
    def __init__(self, nc):
        self.nc = nc; self.ctx = []
    def sb(self, name, shape, dt):
        cm = self.nc.sbuf_tensor(name, shape, dt); t = cm.__enter__(); self.ctx.append(cm); return t
    def ps(self, name, shape, dt):
        cm = self.nc.psum_tensor(name, shape, dt); t = cm.__enter__(); self.ctx.append(cm); return t


class StopEmit(Exception):
    pass

def chk(name):
    if os.environ.get("STOP") == name:
        raise StopEmit()

def emit_p1(nc, sc, al, NB, d):
    sb, ps = al.sb, al.ps
    Wb = sb("Wb", [128, 8, INW], BF16)
    Wa = sb("Wa", [128, 4, 1024], BF16); Wbb = sb("Wbb", [128, 4, 1024], BF16)
    bm = sb("bm", [1, 3072], BF16)
    gb = sb("gb", [128, 1024], F32); gvb = sb("gvb", [128, 512], F32); sinkb = sb("sinkb", [128, 8], F32)
    bsT = sb("bsT", [128, 4], F32); bsb = sb("bsb", [128, 512], F32)
    WsT = sb("WsT", [128, 4, 128], BF16)
    ident = sb("ident", [128, 128], BF16); triu = sb("triu", [128, 128], BF16); ones = sb("ones", [1, 128], BF16)
    mstd = sb("mstd", [128, 256], F32); mfirst = sb("mfirst", [128, 256], F32)
    xb = [sb(f"xb{i}", [128, 1024], F32) for i in range(2)]
    cst = [sb(f"cst{i}", [128, 64], F32) for i in range(2)]
    F1 = sb("F1", [128, 1024], F32); F2 = sb("F2", [128, 1024], F32); F3 = sb("F3", [128, 1024], F32)
    B1 = sb("B1", [128, 1024], BF16); B2 = sb("B2", [128, 1024], BF16); B3 = sb("B3", [128, 1024], BF16)
    B4 = sb("B4", [128, 1024], BF16); B5 = sb("B5", [128, 1024], BF16)
    hT = sb("hT", [128, 8, 128], BF16)
    kT = [sb(f"kT{i}", [128, 4, 128], BF16) for i in range(2)]
    kpad = sb("kpad", [128, 512], BF16)
    vpar = [sb(f"vp{i}", [128, 128], BF16) for i in range(2)]
    small = sb("small", [128, 64], F32)
    PT = ps("PT", [128, 1024], BF16)
    S2 = ps("S2", [128, 4, 256], F32)
    OB = ps("OB", [128, 512], F32)
    PJ = [ps(f"PJ{i}", [128, 512], F32) for i in range(4)]
    pj_i = [0]

    def nextpj():
        i = pj_i[0] % 4; pj_i[0] += 1
        return PJ[i], ("PJ", i)

    for c in range(8):
        for hf in range(2):
            sc.dma("pool", Wb[:, c, hf * 3968:(hf + 1) * 3968], d["w_in"][c * 128:(c + 1) * 128, hf * 3968:(hf + 1) * 3968],
                   writes=["Wb"])
    for c in range(4):
        sc.dma("pool", Wa[:, c, :], d["w_a"][c * 128:(c + 1) * 128, :], writes=["Wa"])
        sc.dma("pool", Wbb[:, c, :], d["w_b"][c * 128:(c + 1) * 128, :], writes=["Wbb"])
    sc.dma("pool", bm[:], d["b_merge"], writes=["bm"])
    sc.dma("sp", gb[:], d["g_pre"].partition_broadcast(128), writes=["gb"])
    sc.dma("sp", gvb[:], d["gm_gain"].partition_broadcast(128), writes=["gvb"])
    sc.dma("sp", sinkb[:], d["sinks"].partition_broadcast(128), writes=["sinkb"])
    sc.dma("sp", bsT[:], d["gm_b"].rearrange("g t -> t g"), writes=["bsT"], allow_slow_non_contiguous=True)
    sc.dma("sp", ident[:], d["c_ident"], writes=["ident"])
    sc.dma("sp", triu[:], d["c_triu"], writes=["triu"])
    sc.dma("sp", ones[:], d["c_ones"], writes=["ones"])
    sc.dma("sp", mstd[:], d["c_mask_std"], writes=["mstd"])
    sc.dma("sp", mfirst[:], d["c_mask_first"], writes=["mfirst"])
    for g in range(4):
        sc.op("dve", lambda e: e.tensor_copy(out=bsb[:, g * 128:(g + 1) * 128], in_=bsT[:, g:g + 1].to_broadcast([128, 128])),
              reads=["bsT"], writes=["bsb"])
    sc.dma("sp", F1[:, 0:512].rearrange("p (g s) -> p g s", g=4), d["gm_w"].rearrange("g t s -> t g s"), writes=["F1"])
    sc.op("dve", lambda e: e.tensor_copy(out=B1[:, 0:512], in_=F1[:, 0:512]), reads=["F1"], writes=["B1"])
    for g in range(4):
        sc.op("pe", lambda e: e.transpose(PT[:, g * 128:(g + 1) * 128], B1[:, g * 128:(g + 1) * 128], ident[:]),
              reads=["B1", "ident"], writes=["PT"])
    for g in range(4):
        sc.op("dve", lambda e: e.tensor_tensor(out=WsT[:, g, :], in0=PT[:, g * 128:(g + 1) * 128], in1=triu[:], op=ALU.mult),
              reads=["triu"], writes=["WsT", "PT"])

    sc.op("pool", lambda e: e.memset(kpad[:], 0.0), writes=["kpad"])
    chk("setup")
    def proj(off, ncols, bias_off=None):
        bank, key = nextpj()
        for c in range(8):
            sc.op("pe", lambda e: e.matmul(bank[:, 0:ncols], lhsT=hT[:, c, :], rhs=Wb[:, c, off:off + ncols],
                                           start=(c == 0), stop=(c == 7 and bias_off is None)),
                  reads=["hT", "Wb"], writes=[key])
        if bias_off is not None:
            sc.op("pe", lambda e: e.matmul(bank[:, 0:ncols], lhsT=ones[0:1, :], rhs=bm[0:1, bias_off:bias_off + ncols],
                                           start=False, stop=True), reads=["ones", "bm"], writes=[key])
        return bank, key

    def transposes(src, nchunk, srckey, dst, dstkey):
        for c in range(nchunk):
            sc.op("pe", lambda e: e.transpose(PT[:, c * 128:(c + 1) * 128], src[:, c * 128:(c + 1) * 128], ident[:]),
                  reads=[srckey, "ident"], writes=["PT"])
        sc.op("act", lambda e: e.copy(out=dst.rearrange("p c t -> p (c t)") if len(dst.shape) == 3 else dst,
                                      in_=PT[:, 0:nchunk * 128]), writes=[dstkey, "PT"])

    def rope(eng, src, nh, dst, cs, srckey, dstkey, tmpa, tmpb, ka, kb_):
        sv = src.rearrange("p (h two d) -> p h two d", two=2, d=32)
        dv = dst.rearrange("p (h two d) -> p h two d", two=2, d=32)
        cosb = cs[:, 0:32].unsqueeze(1).to_broadcast([128, nh, 32])
        sinb = cs[:, 32:64].unsqueeze(1).to_broadcast([128, nh, 32])
        ta = tmpa[:, 0:nh * 32].rearrange("p (h d) -> p h d", d=32)
        tb = tmpb[:, 0:nh * 32].rearrange("p (h d) -> p h d", d=32)
        x1, x2 = sv[:, :, 0, :], sv[:, :, 1, :]
        o = lambda f, r, w: sc.op(eng, f, reads=r, writes=w)
        o(lambda e: e.tensor_tensor(out=ta, in0=x1, in1=cosb, op=ALU.mult), [srckey, "cst"], [ka])
        o(lambda e: e.tensor_tensor(out=tb, in0=x2, in1=sinb, op=ALU.mult), [srckey, "cst"], [kb_])
        o(lambda e: e.tensor_tensor(out=dv[:, :, 0, :], in0=ta, in1=tb, op=ALU.subtract), [ka, kb_], [dstkey])
        o(lambda e: e.tensor_tensor(out=ta, in0=x2, in1=cosb, op=ALU.mult), [srckey, "cst"], [ka])
        o(lambda e: e.tensor_tensor(out=tb, in0=x1, in1=sinb, op=ALU.mult), [srckey, "cst"], [kb_])
        o(lambda e: e.tensor_tensor(out=dv[:, :, 1, :], in0=ta, in1=tb, op=ALU.add), [ka, kb_], [dstkey])

    for n in range(NB + 1):
        par = n % 2
        r0 = n * 128
        t0 = (n - 1) * 128
        X = xb[par]; CS = cst[par]
        sc.dma("sp", X[:], d["x_ext"][r0:r0 + 128, :], writes=[("xb", par)])
        sc.dma("sp", CS[:], d["cs"][r0:r0 + 128, :], writes=["cst"])
        ss = small[:, 0:1]; rs = small[:, 1:2]
        sc.op("act", lambda e: e.activation(out=B1[:], in_=X[:], func=AF.Square, accum_out=ss),
              reads=[("xb", par)], writes=["B1", "ss"])
        sc.op("act", lambda e: e.activation(out=rs, in_=ss, func=AF.Ln, scale=1.0 / D, bias=EPS), reads=["ss"], writes=["rs"])
        sc.op("act", lambda e: e.activation(out=rs, in_=rs, func=AF.Exp, scale=-0.5), reads=["rs"], writes=["rs"])
        sc.op("dve", lambda e: e.scalar_tensor_tensor(out=B2[:], in0=X[:], scalar=rs, in1=gb[:], op0=ALU.mult, op1=ALU.mult),
              reads=[("xb", par), "rs", "gb"], writes=["B2"])
        chk("norm")
        transposes(B2, 8, "B2", hT, "hT")
        chk("h")
        bank, key = proj(KB_OFF, 256)
        sc.op("act", lambda e: e.copy(out=F3[:, 0:128], in_=bank[:, 0:128]), writes=["F3k", key])
        sc.op("act", lambda e: e.copy(out=vpar[par][:], in_=bank[:, 128:256]), writes=[("vp", par), key])
        rope("pool", F3[:, 0:128], 2, F3[:, 128:256], CS, "F3k", "F3kr", F3[:, 256:384], F3[:, 384:512], "F3ta", "F3tb")
        kpv = kpad.rearrange("p (g h c) -> p g h c", g=2, h=2)
        for hf in range(2):
            sc.op("pool", lambda e: e.tensor_copy(out=kpv[:, :, hf, hf * 64:(hf + 1) * 64], in_=F3[:, 128:256].rearrange("p (g d) -> p g d", g=2)),
                  reads=["F3kr"], writes=["kpad"])
        for g in range(4):
            sc.op("pe", lambda e: e.transpose(PT[:, g * 128:(g + 1) * 128], kpad[:, g * 128:(g + 1) * 128], ident[:]),
                  reads=["kpad", "ident"], writes=["PT"])
        sc.op("act", lambda e: e.copy(out=kT[par].rearrange("p g t -> p (g t)"), in_=PT[:, 0:512]), writes=[("kT", par), "PT"])
        if n == 0:
            continue
        chk("kv")
        bank, key = proj(QB_OFF, 512)
        sc.op("act", lambda e: e.mul(out=F1[:, 0:512], in_=bank[:, :], mul=0.125), writes=["F1", key]); chk("B0")
        rope("pool", F1[:, 0:512], 8, B3[:, 256:768], CS, "F1", "B3", F2[:, 0:256], F2[:, 256:512], "F2", "F2")
        qT = B4[:, 0:512].rearrange("p (c t) -> p c t", c=4)
        transposes(B3[:, 256:768], 4, "B3", qT, "qT")
        chk("Ba")
        bank, key = proj(GB_OFF, 512)
        sgb = B5[:, 0:512]
        sc.op("act", lambda e: e.activation(out=sgb, in_=bank[:, :], func=AF.Silu), writes=["sgb", key])
        chk("Bs")
        yb = B5[:, 512:1024]
        msk = mfirst if n == 1 else mstd
        for hb in range(2):
            g = hb
            for j in range(4):
                h = 4 * hb + j; ph = (h % 2) * 64
                sc.op("pe", lambda e: e.matmul(S2[:, j, 0:128], lhsT=qT[:, h // 2, :], rhs=kT[1 - par][:, g * 2 + h % 2, :],
                                               start=True, stop=True), reads=["qT", ("kT", 1 - par)], writes=["S2"])
                sc.op("pe", lambda e: e.matmul(S2[:, j, 128:256], lhsT=qT[:, h // 2, :], rhs=kT[par][:, g * 2 + h % 2, :],
                                               start=True, stop=True), reads=["qT", ("kT", par)], writes=["S2"])
            chk("Bb")
            sm = F2.rearrange("p (j k) -> p j k", j=4)
            sc.op("dve", lambda e: e.tensor_tensor(out=sm, in0=S2[:, :, :], in1=msk[:, :].unsqueeze(1).to_broadcast([128, 4, 256]), op=ALU.add),
                  reads=["mstd", "mfirst"], writes=["F2", "S2"])
            chk("Bc")
            mx = small[:, 4:8]; nmx = small[:, 8:12]; rsum = small[:, 12:16]; es = small[:, 16:20]; den = small[:, 20:24]
            sc.op("dve", lambda e: e.reduce_max(out=mx, in_=sm, axis=AX.X), reads=["F2"], writes=["mx"])
            sc.op("dve", lambda e: e.tensor_tensor(out=mx, in0=mx, in1=sinkb[:, 4 * hb:4 * hb + 4], op=ALU.max), reads=["mx", "sinkb"], writes=["mx"])
            sc.op("dve", lambda e: e.tensor_scalar(out=nmx, in0=mx, scalar1=-1.0, scalar2=None, op0=ALU.mult), reads=["mx"], writes=["nmx"])
            sc.op("dve", lambda e: e.tensor_tensor(out=es, in0=sinkb[:, 4 * hb:4 * hb + 4], in1=nmx, op=ALU.add), reads=["nmx", "sinkb"], writes=["es"])
            chk("Bd")
            p = B1.rearrange("p (j k) -> p j k", j=4)
            for j in range(4):
                sc.op("act", lambda e: e.activation(out=p[:, j, :], in_=sm[:, j, :], func=AF.Exp, bias=nmx[:, j:j + 1], accum_out=rsum[:, j:j + 1]),
                      reads=["F2", "nmx"], writes=["B1", "rsum"])
            sc.op("act", lambda e: e.activation(out=es, in_=es, func=AF.Exp), reads=["es"], writes=["es"])
            sc.op("dve", lambda e: e.tensor_tensor(out=den, in0=rsum, in1=es, op=ALU.add), reads=["rsum", "es"], writes=["den"])
            sc.op("dve", lambda e: e.reciprocal(out=den, in_=den), reads=["den"], writes=["den"])
            chk("Be")
            for j in range(4):
                for kb in range(2):
                    sc.op("pe", lambda e: e.transpose(PT[:, (j * 2 + kb) * 128:(j * 2 + kb + 1) * 128], p[:, j, kb * 128:(kb + 1) * 128], ident[:]),
                          reads=["B1", "ident"], writes=["PT"])
            pT = B2.rearrange("p (c t) -> p c t", c=8)
            sc.op("act", lambda e: e.copy(out=B2[:], in_=PT[:, :]), writes=["B2", "PT"])
            for j in range(4):
                sc.op("pe", lambda e: e.matmul(OB[:, j * 64:(j + 1) * 64], lhsT=pT[:, j * 2, :], rhs=vpar[1 - par][:, g * 64:(g + 1) * 64],
                                               start=True, stop=False), reads=["B2", ("vp", 1 - par)], writes=["OB"])
                sc.op("pe", lambda e: e.matmul(OB[:, j * 64:(j + 1) * 64], lhsT=pT[:, j * 2 + 1, :], rhs=vpar[par][:, g * 64:(g + 1) * 64],
                                               start=False, stop=True), reads=["B2", ("vp", par)], writes=["OB"])
            chk("Bf")
            obs = F3[:, 512:768].rearrange("p (j d) -> p j d", j=4)
            sc.op("dve", lambda e: e.tensor_tensor(out=obs, in0=OB[:, 0:256].rearrange("p (j d) -> p j d", j=4),
                                                   in1=den.unsqueeze(2).to_broadcast([128, 4, 64]), op=ALU.mult),
                  reads=["den"], writes=["F3o", "OB"])
            sc.op("dve", lambda e: e.tensor_tensor(out=yb[:, hb * 256:(hb + 1) * 256], in0=F3[:, 512:768], in1=sgb[:, hb * 256:(hb + 1) * 256], op=ALU.mult),
                  reads=["F3o", "sgb"], writes=["yb"])
        ybT = B4[:, 512:1024].rearrange("p (c t) -> p c t", c=4)
        transposes(yb, 4, "yb", ybT, "ybT")
        chk("B")
        bank, key = proj(U_OFF, 512)
        uf = F1[:, 512:1024]
        sc.op("act", lambda e: e.copy(out=uf, in_=bank[:, :]), writes=["uf", key])
        bank, key = proj(V_OFF, 512)
        st = small[:, 24:30]; mv = small[:, 30:32]; rstd = small[:, 32:33]
        sc.op("dve", lambda e: e.bn_stats(out=st, in_=bank[:, :]), writes=["st", key])
        sc.op("dve", lambda e: e.bn_aggr(out=mv, in_=st), reads=["st"], writes=["mv"])
        sc.op("act", lambda e: e.activation(out=rstd, in_=mv[:, 1:2], func=AF.Ln, bias=EPS), reads=["mv"], writes=["rstd"])
        sc.op("act", lambda e: e.activation(out=rstd, in_=rstd, func=AF.Exp, scale=-0.5), reads=["rstd"], writes=["rstd"])
        sc.op("dve", lambda e: e.tensor_scalar(out=F2[:, 0:512], in0=bank[:, :], scalar1=mv[:, 0:1], scalar2=rstd, op0=ALU.subtract, op1=ALU.mult),
              reads=["mv", "rstd"], writes=["F2", key])
        vn = B1[:, 0:512]
        sc.op("dve", lambda e: e.tensor_tensor(out=vn, in0=F2[:, 0:512], in1=gvb[:], op=ALU.mult), reads=["F2", "gvb"], writes=["B1"])
        bank, key = proj(GA_OFF, 512)
        sga = B1[:, 512:1024]
        sc.op("act", lambda e: e.activation(out=sga, in_=bank[:, :], func=AF.Silu), writes=["B1", key])
        mb, mkey = nextpj()
        for g in range(4):
            sc.op("pe", lambda e: e.matmul(mb[:, g * 128:(g + 1) * 128], lhsT=WsT[:, g, :], rhs=vn[:, g * 128:(g + 1) * 128], start=True, stop=True),
                  reads=["WsT", "B1"], writes=[mkey])
        sc.op("dve", lambda e: e.tensor_tensor(out=F2[:, 0:512], in0=mb[:, :], in1=bsb[:], op=ALU.add), reads=["bsb"], writes=["F2", mkey])
        sc.op("dve", lambda e: e.tensor_tensor(out=F2[:, 0:512], in0=F2[:, 0:512], in1=uf, op=ALU.mult), reads=["F2", "uf"], writes=["F2"])
        ya = B3[:, 0:512]
        sc.op("dve", lambda e: e.tensor_tensor(out=ya, in0=F2[:, 0:512], in1=sga, op=ALU.mult), reads=["F2", "B1"], writes=["B3"])
        yaT = B2[:, 0:512].rearrange("p (c t) -> p c t", c=4)
        transposes(ya, 4, "B3", yaT, "B2")
        chk("A")
        g0 = F2; g1 = F3
        G2 = B3
        for i, (dst, dkey) in enumerate([(g0, "F2"), (g1, "F3all"), (G2, "B3")]):
            for hf in range(2):
                bank, key = proj(MG_OFF + i * 1024 + hf * 512, 512, bias_off=i * 1024 + hf * 512)
                wr = [dkey, key] + (F3ALL if dkey == "F3all" else [])
                sc.op("act", lambda e: e.activation(out=dst[:, hf * 512:(hf + 1) * 512], in_=bank[:, :], func=AF.Sigmoid), writes=wr)
        sc.dma("sp", d["g2"][t0:t0 + 128, :], G2[:], reads=["B3"], writes=["g2_d"], key="B3")
        chk("gates")
        pa = F1
        for hf in range(2):
            bank, key = nextpj()
            for c in range(4):
                sc.op("pe", lambda e: e.matmul(bank[:, :], lhsT=yaT[:, c, :], rhs=Wa[:, c, hf * 512:(hf + 1) * 512], start=(c == 0), stop=(c == 3)),
                      reads=["B2", "Wa"], writes=[key])
            sc.op("dve", lambda e: e.tensor_tensor(out=pa[:, hf * 512:(hf + 1) * 512], in0=bank[:, :], in1=g0[:, hf * 512:(hf + 1) * 512], op=ALU.mult),
                  reads=["F2"], writes=["F1", "uf", key])
        for hf in range(2):
            bank, key = nextpj()
            for c in range(4):
                sc.op("pe", lambda e: e.matmul(bank[:, :], lhsT=ybT[:, c, :], rhs=Wbb[:, c, hf * 512:(hf + 1) * 512], start=(c == 0), stop=(c == 3)),
                      reads=["ybT", "Wbb"], writes=[key])
            sc.op("dve", lambda e: e.tensor_tensor(out=g1[:, hf * 512:(hf + 1) * 512], in0=bank[:, :], in1=g1[:, hf * 512:(hf + 1) * 512], op=ALU.mult),
                  reads=F3ALL, writes=F3ALL + [key])
        sc.op("dve", lambda e: e.tensor_tensor(out=pa[:], in0=pa[:], in1=g1[:], op=ALU.add), reads=F3ALL + ["F1", "uf"], writes=["F1", "uf"])
        sc.dma("sp", d["partial"][t0:t0 + 128, :], pa[:], reads=["F1", "uf"], writes=["partial_d"], key="F1")
        chk("partial")
        for qi, (off, func, scale, dkey) in enumerate([(QC_OFF, AF.Copy, 128 ** -0.5, "qcT"), (KC_OFF, AF.Copy, 1.0, "kcT"), (GC_OFF, AF.Silu, 1.0, "sgcT")]):
            bank, key = nextpj()
            for hh in range(4):
                for c in range(8):
                    sc.op("pe", lambda e: e.matmul(bank[:, hh * 128:(hh + 1) * 128], lhsT=Wb[:, c, off + hh * 128:off + (hh + 1) * 128], rhs=hT[:, c, :],
                                                   start=(c == 0), stop=(c == 7)), reads=["hT", "Wb"], writes=[key])
            stg = B5[:, 0:512] if qi != 1 else B4[:, 0:512]
            skey = "sgb" if qi != 1 else "qT"
            if func == AF.Copy:
                sc.op("act", lambda e: e.mul(out=stg, in_=bank[:, :], mul=scale), writes=[skey, key])
            else:
                sc.op("act", lambda e: e.activation(out=stg, in_=bank[:, :], func=func), writes=[skey, key])
            sc.dma("sp", d[dkey].rearrange("(h p) t -> p h t", p=128)[:, :, t0:t0 + 128], stg.rearrange("p (h t) -> p h t", h=4),
                   reads=[skey], writes=[dkey + "_d"], key=skey)
        bank, key = proj(VC_OFF, 512)
        sc.op("act", lambda e: e.copy(out=B4[:, 512:1024], in_=bank[:, :]), writes=["ybT", key])
        sc.dma("sp", d["vc"][t0:t0 + 128, :], B4[:, 512:1024], reads=["ybT"], writes=["vc_d"], key="ybT")


P1_IN = {"x_ext": None, "w_in": [1024, INW], "gm_w": [4, 128, 128], "gm_b": [4, 128], "gm_gain": [512], "sinks": [8],
         "w_a": [512, 1024], "w_b": [512, 1024], "b_merge": [1, 3072], "g_pre": [1024], "cs": None}


def build_p1(NB):
    nc = bass.Bass("TRN2", target_bir_lowering=False)
    T = NB * 128
    d = {}
    for k, shp in P1_IN.items():
        if k == "x_ext": shp = [T + 128, 1024]
        if k == "cs": shp = [T + 128, 64]
        d[k] = nc.dram_tensor(k, shp, F32, kind="ExternalInput").ap()
    for k, a in tok_consts().items():
        d[k] = nc.dram_tensor(k, list(a.shape), BF16 if a.dtype == bf else F32, kind="ExternalInput").ap()
    d["partial"] = nc.dram_tensor("partial", [T, 1024], F32, kind="ExternalOutput").ap()
    d["g2"] = nc.dram_tensor("g2", [T, 1024], BF16, kind="ExternalOutput").ap()
    for k in ["sgcT", "qcT", "kcT"]:
        d[k] = nc.dram_tensor(k, [512, T], BF16, kind="ExternalOutput").ap()
    d["vc"] = nc.dram_tensor("vc", [T, 512], BF16, kind="ExternalOutput").ap()
    al = Alloc(nc); sc = Sched(nc)
    try:
        emit_p1(nc, sc, al, NB, d)
    except StopEmit:
        print("stopped early")
    sc.finish("sp")
    return nc, sc


def emit_p3(nc, sc, al, NB, d):
    sb, ps = al.sb, al.ps
    Wc = sb("Wc", [128, 4, 1024], BF16); Wo = sb("Wo", [128, 8, 1024], BF16)
    gpb = sb("gpb", [128, 1024], F32); ident = sb("ident3", [128, 128], BF16)
    xb = [sb(f"x3_{i}", [128, 1024], F32) for i in range(2)]
    pb = [sb(f"p3_{i}", [128, 1024], F32) for i in range(2)]
    g2b = [sb(f"g3_{i}", [128, 1024], BF16) for i in range(2)]
    oTb = [sb(f"o3_{i}", [128, 4, 128], BF16) for i in range(2)]
    sgb = [sb(f"s3_{i}", [128, 4, 128], BF16) for i in range(2)]
    ycT = sb("ycT", [128, 4, 128], BF16)
    M1 = sb("M1", [128, 1024], F32); mb = sb("mb3", [128, 1024], BF16); mT = sb("mT", [128, 8, 128], BF16)
    junk = sb("junk3", [128, 512], BF16); small = sb("small3", [128, 8], F32)
    Y = sb("Y3", [128, 1024], F32)
    PT = ps("PT3", [128, 1024], BF16)
    PJ = [ps(f"PJ3_{i}", [128, 512], F32) for i in range(4)]
    for c in range(4):
        sc.dma("pool", Wc[:, c, :], d["w_c"][c * 128:(c + 1) * 128, :], writes=["Wc"])
    for c in range(8):
        sc.dma("pool", Wo[:, c, :], d["w_out"][c * 128:(c + 1) * 128, :], writes=["Wo"])
    sc.dma("sp", gpb[:], d["g_post"].partition_broadcast(128), writes=["gpb"])
    sc.dma("sp", ident[:], d["c_ident"], writes=["ident"])
    for n in range(NB):
        par = n % 2; t0 = n * 128
        sc.dma("sp", xb[par][:], d["x"][t0:t0 + 128, :], writes=[("x", par)])
        sc.dma("act", pb[par][:], d["partial"][t0:t0 + 128, :], writes=[("p", par)])
        sc.dma("sp", g2b[par][:], d["g2"][t0:t0 + 128, :], writes=[("g2", par)])
        sc.dma("act", oTb[par][:], d["oT"].rearrange("(h p) t -> p h t", p=128)[:, :, t0:t0 + 128], writes=[("oT", par)])
        sc.dma("sp", sgb[par][:], d["sgcT"].rearrange("(h p) t -> p h t", p=128)[:, :, t0:t0 + 128], writes=[("sg", par)])
        sc.op("pool", lambda e: e.tensor_tensor(out=ycT[:], in0=oTb[par][:], in1=sgb[par][:], op=ALU.mult),
              reads=[("oT", par), ("sg", par)], writes=["ycT"])
        for hf in range(2):
            bank = PJ[hf]
            for c in range(4):
                sc.op("pe", lambda e: e.matmul(bank[:, :], lhsT=ycT[:, c, :], rhs=Wc[:, c, hf * 512:(hf + 1) * 512], start=(c == 0), stop=(c == 3)),
                      reads=["ycT", "Wc"], writes=[("PJ", hf)])
            sc.op("dve", lambda e: e.tensor_tensor(out=M1[:, hf * 512:(hf + 1) * 512], in0=bank[:, :], in1=g2b[par][:, hf * 512:(hf + 1) * 512], op=ALU.mult),
                  reads=[("g2", par)], writes=["M1", ("PJ", hf)])
        sc.op("dve", lambda e: e.tensor_tensor(out=mb[:], in0=M1[:], in1=pb[par][:], op=ALU.add), reads=["M1", ("p", par)], writes=["mb"])
        for c in range(8):
            sc.op("pe", lambda e: e.transpose(PT[:, c * 128:(c + 1) * 128], mb[:, c * 128:(c + 1) * 128], ident[:]), reads=["mb", "ident"], writes=["PT"])
        sc.op("act", lambda e: e.copy(out=mT.rearrange("p c t -> p (c t)"), in_=PT[:, :]), writes=["mT", "PT"])
        for hf in range(2):
            bank = PJ[2 + hf]
            for c in range(8):
                sc.op("pe", lambda e: e.matmul(bank[:, :], lhsT=mT[:, c, :], rhs=Wo[:, c, hf * 512:(hf + 1) * 512], start=(c == 0), stop=(c == 7)),
                      reads=["mT", "Wo"], writes=[("PJ", 2 + hf)])
            sc.op("act", lambda e: e.activation(out=junk[:], in_=bank[:, :], func=AF.Square, accum_out=small[:, hf:hf + 1]),
                  writes=["junk", ("ss", hf), ("PJ", 2 + hf)])
        rs = small[:, 2:3]
        sc.op("dve", lambda e: e.tensor_tensor(out=rs, in0=small[:, 0:1], in1=small[:, 1:2], op=ALU.add), reads=[("ss", 0), ("ss", 1)], writes=["rs"])
        sc.op("act", lambda e: e.activation(out=rs, in_=rs, func=AF.Ln, scale=1.0 / D, bias=EPS), reads=["rs"], writes=["rs"])
        sc.op("act", lambda e: e.activation(out=rs, in_=rs, func=AF.Exp, scale=-0.5), reads=["rs"], writes=["rs"])
        for hf in range(2):
            bank = PJ[2 + hf]
            sc.op("dve", lambda e: e.scalar_tensor_tensor(out=Y[:, hf * 512:(hf + 1) * 512], in0=bank[:, :], scalar=rs, in1=gpb[:, hf * 512:(hf + 1) * 512],
                                                          op0=ALU.mult, op1=ALU.mult), reads=["rs", "gpb"], writes=["Y", ("PJ", 2 + hf)])
        sc.op("dve", lambda e: e.tensor_tensor(out=Y[:], in0=Y[:], in1=xb[par][:], op=ALU.add), reads=["Y", ("x", par)], writes=["Y"])
        sc.dma("sp", d["xo"][t0:t0 + 128, :], Y[:], reads=["Y"], writes=["xo_d"], key="Y")


def build_p3(NB):
    nc = bass.Bass("TRN2", target_bir_lowering=False)
    T = NB * 128
    d = {}
    d["x"] = nc.dram_tensor("x", [T, 1024], F32, kind="ExternalInput").ap()
    d["partial"] = nc.dram_tensor("partial", [T, 1024], F32, kind="ExternalInput").ap()
    d["g2"] = nc.dram_tensor("g2", [T, 1024], BF16, kind="ExternalInput").ap()
    d["sgcT"] = nc.dram_tensor("sgcT", [512, T], BF16, kind="ExternalInput").ap()
    d["oT"] = nc.dram_tensor("oT", [512, T], BF16, kind="ExternalInput").ap()
    d["w_c"] = nc.dram_tensor("w_c", [512, 1024], F32, kind="ExternalInput").ap()
    d["w_out"] = nc.dram_tensor("w_out", [1024, 1024], F32, kind="ExternalInput").ap()
    d["g_post"] = nc.dram_tensor("g_post", [1024], F32, kind="ExternalInput").ap()
    d["c_ident"] = nc.dram_tensor("c_ident", [128, 128], BF16, kind="ExternalInput").ap()
    d["xo"] = nc.dram_tensor("xo", [T, 1024], F32, kind="ExternalOutput").ap()
    al = Alloc(nc); sc = Sched(nc)
    emit_p3(nc, sc, al, NB, d)
    sc.finish("sp")
    return nc, sc


_CACHE = {}


def _get(name, fn):
    if name not in _CACHE:
        _CACHE[name] = fn()
    return _CACHE[name]


def kernel(x, w_in, gm_w_s, gm_b_s, gm_norm_gain, sw_sinks, w_branch_a, w_branch_b, w_branch_c, b_merge, w_out,
           g_pre, g_post):
    f32 = np.float32
    x = np.asarray(x, f32)
    A = lambda a: np.ascontiguousarray(np.asarray(a, f32))
    w_in, gm_w_s, gm_b_s, gm_norm_gain, sw_sinks = A(w_in), A(gm_w_s), A(gm_b_s), A(gm_norm_gain), A(sw_sinks)
    w_branch_a, w_branch_b, w_branch_c, b_merge, w_out = A(w_branch_a), A(w_branch_b), A(w_branch_c), A(b_merge), A(w_out)
    g_pre, g_post = A(g_pre), A(g_post)
    Bn, S, Dm = x.shape
    NQ = 4; T = S // NQ; NB = T // 128; depth = w_in.shape[0]
    cores = list(range(8))
    tc = tok_consts(); c2 = p2_consts()
    cs_tabs = [rope_table(np.arange(q * T - 128, (q + 1) * T)) for q in range(NQ)]
    xs = [np.ascontiguousarray(x[c // 4, (c % 4) * T:(c % 4 + 1) * T]) for c in cores]
    for l in range(depth):
        nc1, _ = build_p1(NB)
        in_maps = []
        for c in cores:
            b, q = divmod(c, 4)
            halo = xs[c - 1][-128:] if q > 0 else np.zeros((128, Dm), f32)
            m = {"x_ext": np.ascontiguousarray(np.concatenate([halo, xs[c]], 0)), "w_in": w_in[l], "gm_w": gm_w_s[l], "gm_b": gm_b_s[l],
                 "gm_gain": gm_norm_gain[l], "sinks": sw_sinks[l], "w_a": w_branch_a[l], "w_b": w_branch_b[l],
                 "b_merge": np.ascontiguousarray(b_merge[l].reshape(1, 3072)), "g_pre": g_pre[l], "cs": cs_tabs[q]}
            m.update(tc)
            if q > 0:
                m["c_mask_first"] = tc["c_mask_std"]
            in_maps.append(m)
        r1 = run_bass_kernel_spmd(nc1, in_maps, core_ids=cores).results
        nc2, _ = build_p2(S)
        in_maps = []
        for c in cores:
            b, h = divmod(c, 4)
            rows = slice(h * 128, (h + 1) * 128)
            m = {"qT": np.ascontiguousarray(np.concatenate([r1[b * 4 + q]["qcT"][rows] for q in range(NQ)], 1)),
                 "kT": np.ascontiguousarray(np.concatenate([r1[b * 4 + q]["kcT"][rows] for q in range(NQ)], 1)),
                 "v": np.ascontiguousarray(np.concatenate([r1[b * 4 + q]["vc"][:, rows] for q in range(NQ)], 0))}
            m.update(c2)
            in_maps.append(m)
        r2 = run_bass_kernel_spmd(nc2, in_maps, core_ids=cores).results
        nc3, _ = build_p3(NB)
        in_maps = []
        for c in cores:
            b, q = divmod(c, 4)
            m = {"x": xs[c], "partial": r1[c]["partial"], "g2": r1[c]["g2"], "sgcT": r1[c]["sgcT"],
                 "oT": np.ascontiguousarray(np.concatenate([r2[b * 4 + h]["oT"][:, q * T:(q + 1) * T] for h in range(4)], 0)),
                 "w_c": w_branch_c[l], "w_out": w_out[l], "g_post": g_post[l], "c_ident": tc["c_ident"]}
            in_maps.append(m)
        r3 = run_bass_kernel_spmd(nc3, in_maps, core_ids=cores).results
        xs = [np.asarray(r3[c]["xo"], f32) for c in cores]
    out = np.zeros((Bn, S, Dm), f32)
    for c in cores:
        out[c // 4, (c % 4) * T:(c % 4 + 1) * T] = xs[c]
    return out
```
